# Optimizing a Trainium2 kernel written in Bass

```python
import jax, jax.numpy as jnp
from jax import lax
import numpy as np

D_MODEL = 2048
BATCH = 16
SEQ = 2048
DEPTH = 2
DEC_BATCH = 2
DEC_SEQ = 4096
PAST_LEN = 128

GRID_W = 64
N_BRANCH = 4
BR_W = 512
HEAD_DIM = 64
CHUNK = 128
GM_GROUPS = 4
GM_GW = BR_W // GM_GROUPS
SW_HEADS = 8
SW_KV = 2
SW_WIN = 128
SW_BLOCK = 128
ROPE_THETA = 10000.0
CONV_W = 31
NA_HEADS = 8
NA_ROWS = 8
NA_COLS = 16
NA_BLOCK_W = 16
NA_SPAN_W = 2 * NA_COLS
FFN_HIDDEN = 5632
FFN_CONV = 3
EPS = 1e-6
NEG_INF = -1e30

IN_SIZES = (BR_W, BR_W, SW_HEADS * HEAD_DIM, SW_KV * HEAD_DIM, SW_KV * HEAD_DIM,
            BR_W, BR_W, NA_HEADS * HEAD_DIM, NA_HEADS * HEAD_DIM, NA_HEADS * HEAD_DIM,
            N_BRANCH * D_MODEL)
N_IN = sum(IN_SIZES)

kernel_name = "hybrid_gated_parallel_encoder"


def _rmsnorm(x, g):
    xf = x.astype(jnp.float32)
    y = xf * lax.rsqrt(jnp.mean(xf * xf, axis=-1, keepdims=True) + EPS)
    return (y * g.astype(jnp.float32)).astype(x.dtype)


def _layernorm(x, g, b):
    xf = x.astype(jnp.float32)
    mu = jnp.mean(xf, axis=-1, keepdims=True)
    xc = xf - mu
    y = xc * lax.rsqrt(jnp.mean(xc * xc, axis=-1, keepdims=True) + EPS)
    return (y * g.astype(jnp.float32) + b.astype(jnp.float32)).astype(x.dtype)


def _depthwise_conv(x, w, b):
    k = w.shape[0]
    y = lax.conv_general_dilated(x, w[:, None, :].astype(x.dtype), window_strides=(1,),
                                 padding=[(k // 2, k // 2)],
                                 dimension_numbers=('NWC', 'WIO', 'NWC'),
                                 feature_group_count=x.shape[-1])
    return y + b.astype(x.dtype)


def _rope(x, pos):
    half = x.shape[-1] // 2
    inv = jnp.power(ROPE_THETA, -jnp.arange(half, dtype=jnp.float32) / half)
    ang = pos.astype(jnp.float32)[:, None] * inv[None, :]
    cos = jnp.cos(ang)[:, None, :]
    sin = jnp.sin(ang)[:, None, :]
    xf = x.astype(jnp.float32)
    x1, x2 = xf[..., :half], xf[..., half:]
    return jnp.concatenate([x1 * cos - x2 * sin, x2 * cos + x1 * sin], axis=-1).astype(x.dtype)


def _spatial_gating(u, v, ln_g, ln_b, ws, bs):
    B, T, C = v.shape
    vn = _layernorm(v, ln_g, ln_b).reshape(B, T // CHUNK, CHUNK, GM_GROUPS, GM_GW)
    mixed = jnp.einsum('gpq,bnqgc->bnpgc', ws.astype(vn.dtype), vn) + bs.T.astype(vn.dtype)[None, None, :, :, None]
    return u * mixed.reshape(B, T, C)


def _sliding_window_attention(q, k, v, sink):
    B, T, H, dh = q.shape
    nb = T // SW_BLOCK
    g = H // SW_KV
    qb = q.reshape(B, nb, SW_BLOCK, SW_KV, g, dh)
    pad = ((0, 0), (SW_BLOCK, SW_BLOCK), (0, 0), (0, 0))

    def band(a):
        ab = jnp.pad(a, pad).reshape(B, nb + 2, SW_BLOCK, SW_KV, dh)
        return jnp.concatenate([ab[:, :-2], ab[:, 1:-1], ab[:, 2:]], axis=2)

    kb, vb = band(k), band(v)
    start = jnp.arange(nb)[:, None] * SW_BLOCK
    qpos = start + jnp.arange(SW_BLOCK)[None, :]
    kpos = start - SW_BLOCK + jnp.arange(3 * SW_BLOCK)[None, :]
    valid = ((jnp.abs(qpos[:, :, None] - kpos[:, None, :]) <= SW_WIN)
             & (kpos[:, None, :] >= 0) & (kpos[:, None, :] < T))
    s = jnp.einsum('bnqhgd,bnkhd->bnhgqk', qb, kb, preferred_element_type=jnp.float32) * (dh ** -0.5)
    s = jnp.where(valid[None, :, None, None], s, NEG_INF)
    sk = sink.astype(jnp.float32).reshape(SW_KV, g)[:, :, None]
    m = jnp.maximum(jnp.max(s, axis=-1), sk)
    p = jnp.exp(s - m[..., None])
    denom = jnp.sum(p, axis=-1) + jnp.exp(sk - m)
    w = (p / denom[..., None]).astype(v.dtype)
    out = jnp.einsum('bnhgqk,bnkhd->bnqhgd', w, vb)
    return out.reshape(B, T, H * dh)


def _conv_module(a, gate, dw, dwb, ln_g, ln_b):
    x = a * jax.nn.sigmoid(gate)
    x = _depthwise_conv(x, dw, dwb)
    x = _layernorm(x, ln_g, ln_b)
    return jax.nn.silu(x)


def _neighbourhood_attention(q, k, v, rpb):
    B, T, H, dh = q.shape
    rows = T // GRID_W
    wr = min(NA_ROWS, rows)
    ncb = GRID_W // NA_BLOCK_W
    qc = np.arange(GRID_W).reshape(ncb, NA_BLOCK_W)
    cs = np.clip(qc - NA_COLS // 2, 0, GRID_W - NA_COLS)
    ks = np.clip(np.arange(ncb) * NA_BLOCK_W - NA_COLS // 2, 0, GRID_W - NA_SPAN_W)
    kc = ks[:, None] + np.arange(NA_SPAN_W)[None, :]
    col_ok = (kc[:, None, :] >= cs[..., None]) & (kc[:, None, :] < cs[..., None] + NA_COLS)
    dc_idx = np.clip(kc[:, None, :] - qc[..., None], 1 - NA_COLS, NA_COLS - 1) + NA_COLS - 1
    qg = q.reshape(B, rows, GRID_W, H, dh)
    kg = k.reshape(B, rows, GRID_W, H, dh)
    vg = v.reshape(B, rows, GRID_W, H, dh)
    rpb_f = rpb.astype(jnp.float32)
    scale = dh ** -0.5

    def one_row(r):
        rs = jnp.clip(r - wr // 2, 0, rows - wr)
        k_blk = lax.dynamic_slice_in_dim(kg, rs, wr, axis=1)[:, :, kc]
        v_blk = lax.dynamic_slice_in_dim(vg, rs, wr, axis=1)[:, :, kc]
        q_blk = lax.dynamic_index_in_dim(qg, r, axis=1, keepdims=False).reshape(B, ncb, NA_BLOCK_W, H, dh)
        s = jnp.einsum('bcqhd,bwckhd->bhcqwk', q_blk, k_blk, preferred_element_type=jnp.float32) * scale
        dr_idx = rs + jnp.arange(wr) - r + NA_ROWS - 1
        bias = rpb_f[:, dr_idx][:, :, dc_idx].transpose(0, 2, 3, 1, 4)
        s = jnp.where(col_ok[:, :, None, :], s + bias, NEG_INF)
        p = jax.nn.softmax(s.reshape(B, H, ncb, NA_BLOCK_W, wr * NA_SPAN_W), axis=-1).reshape(s.shape)
        o = jnp.einsum('bhcqwk,bwckhd->bcqhd', p.astype(v.dtype), v_blk)
        return o.reshape(B, GRID_W, H, dh)

    out = lax.map(one_row, jnp.arange(rows))
    return out.transpose(1, 0, 2, 3, 4).reshape(B, T, H * dh)


def _conv_ffn(h, w_up, dw, dwb, w_down):
    z = _depthwise_conv(h @ w_up, dw, dwb)
    a, b = jnp.split(z, 2, axis=-1)
    return (jax.nn.silu(a) * b) @ w_down


def _trunk(x, norm1_g, w_in, gate_b, gm_ln_g, gm_ln_b, gm_ws, gm_bs, sw_sink,
           cv_dw, cv_dwb, cv_ln_g, cv_ln_b, na_rpb, w_branch, w_out,
           norm2_g, w_up, ffn_dw, ffn_dwb, w_down, final_g):
    B, T, _ = x.shape
    pos = jnp.arange(T)
    splits = np.cumsum(IN_SIZES)[:-1].tolist()
    for l in range(DEPTH):
        h = _rmsnorm(x, norm1_g[l])
        z = h @ w_in[l]
        (gm_u, gm_v, sw_q, sw_k, sw_v, cv_a, cv_g,
         na_q, na_k, na_v, gates) = jnp.split(z, splits, axis=-1)
        o_a = _spatial_gating(gm_u, gm_v, gm_ln_g[l], gm_ln_b[l], gm_ws[l], gm_bs[l])
        q = _rope(sw_q.reshape(B, T, SW_HEADS, HEAD_DIM), pos)
        k = _rope(sw_k.reshape(B, T, SW_KV, HEAD_DIM), pos)
        o_b = _sliding_window_attention(q, k, sw_v.reshape(B, T, SW_KV, HEAD_DIM), sw_sink[l])
        o_c = _conv_module(cv_a, cv_g, cv_dw[l], cv_dwb[l], cv_ln_g[l], cv_ln_b[l])
        o_d = _neighbourhood_attention(na_q.reshape(B, T, NA_HEADS, HEAD_DIM),
                                       na_k.reshape(B, T, NA_HEADS, HEAD_DIM),
                                       na_v.reshape(B, T, NA_HEADS, HEAD_DIM), na_rpb[l])
        gs = jax.nn.sigmoid(gates.reshape(B, T, N_BRANCH, D_MODEL) + gate_b[l])
        merged = (gs[:, :, 0] * (o_a @ w_branch[l, 0]) + gs[:, :, 1] * (o_b @ w_branch[l, 1])
                  + gs[:, :, 2] * (o_c @ w_branch[l, 2]) + gs[:, :, 3] * (o_d @ w_branch[l, 3]))
        x = x + merged @ w_out[l]
        x = x + _conv_ffn(_rmsnorm(x, norm2_g[l]), w_up[l], ffn_dw[l], ffn_dwb[l], w_down[l])
    return _rmsnorm(x, final_g)


def setup_inputs(seed: int = 0) -> dict:
    key = jax.random.key(seed)
    ks = jax.random.split(key, 26)
    L, D = DEPTH, D_MODEL

    def nrm(k, shape, scale):
        return jax.random.normal(k, shape, jnp.float32) * scale

    return {
        "x_prompt": nrm(ks[0], (BATCH, SEQ, D), 1.0),
        "x_sample": nrm(ks[1], (DEC_BATCH, DEC_SEQ, D), 1.0),
        "norm1_g": 1.0 + nrm(ks[2], (L, D), 0.02),
        "w_in": nrm(ks[3], (L, D, N_IN), D ** -0.5),
        "gate_b": nrm(ks[4], (L, N_BRANCH, D), 0.1),
        "gm_ln_g": 1.0 + nrm(ks[5], (L, BR_W), 0.02),
        "gm_ln_b": nrm(ks[6], (L, BR_W), 0.02),
        "gm_ws": nrm(ks[7], (L, GM_GROUPS, CHUNK, CHUNK), CHUNK ** -0.5),
        "gm_bs": 1.0 + nrm(ks[8], (L, GM_GROUPS, CHUNK), 0.02),
        "sw_sink": nrm(ks[9], (L, SW_HEADS), 0.5),
        "cv_dw": nrm(ks[10], (L, CONV_W, BR_W), CONV_W ** -0.5),
        "cv_dwb": nrm(ks[11], (L, BR_W), 0.02),
        "cv_ln_g": 1.0 + nrm(ks[12], (L, BR_W), 0.02),
        "cv_ln_b": nrm(ks[13], (L, BR_W), 0.02),
        "na_rpb": nrm(ks[14], (L, NA_HEADS, 2 * NA_ROWS - 1, 2 * NA_COLS - 1), 0.1),
        "w_branch": nrm(ks[15], (L, N_BRANCH, BR_W, D), BR_W ** -0.5),
        "w_out": nrm(ks[16], (L, D, D), D ** -0.5),
        "norm2_g": 1.0 + nrm(ks[17], (L, D), 0.02),
        "w_up": nrm(ks[18], (L, D, 2 * FFN_HIDDEN), D ** -0.5),
        "ffn_dw": nrm(ks[19], (L, FFN_CONV, 2 * FFN_HIDDEN), FFN_CONV ** -0.5),
        "ffn_dwb": nrm(ks[20], (L, 2 * FFN_HIDDEN), 0.02),
        "w_down": nrm(ks[21], (L, FFN_HIDDEN, D), FFN_HIDDEN ** -0.5),
        "final_g": 1.0 + nrm(ks[22], (D,), 0.02),
    }


def reference(x_prompt, x_sample, norm1_g, w_in, gate_b, gm_ln_g, gm_ln_b, gm_ws, gm_bs,
              sw_sink, cv_dw, cv_dwb, cv_ln_g, cv_ln_b, na_rpb, w_branch, w_out,
              norm2_g, w_up, ffn_dw, ffn_dwb, w_down, final_g):
    y_prompt = _trunk(x_prompt, norm1_g, w_in, gate_b, gm_ln_g, gm_ln_b, gm_ws, gm_bs, sw_sink,
                      cv_dw, cv_dwb, cv_ln_g, cv_ln_b, na_rpb, w_branch, w_out,
                      norm2_g, w_up, ffn_dw, ffn_dwb, w_down, final_g)
    y_sample = _trunk(x_sample, norm1_g, w_in, gate_b, gm_ln_g, gm_ln_b, gm_ws, gm_bs, sw_sink,
                      cv_dw, cv_dwb, cv_ln_g, cv_ln_b, na_rpb, w_branch, w_out,
                      norm2_g, w_up, ffn_dw, ffn_dwb, w_down, final_g)
    return (y_prompt, y_sample)
```

```python
import os
import numpy as np
from contextlib import ExitStack
import concourse.bass as bass
import concourse.mybir as mybir
from concourse.bass_utils import run_bass_kernel_spmd

F32 = mybir.dt.float32
BF16 = mybir.dt.bfloat16
AF = mybir.ActivationFunctionType
ALU = mybir.AluOpType

T = 2048
D = 2048
NT = 4
TT = 512
NCW = 103
CH_U, CH_V, CH_Q, CH_QP, CH_K, CH_KP, CH_SV, CH_CA, CH_CG, CH_NQ, CH_NK, CH_NV, CH_G = 0, 4, 8, 12, 16, 17, 18, 19, 23, 27, 31, 35, 39
EPS = 1e-6
NEG = -30000.0
PERM_HEADS = [0, 4, 1, 5, 2, 6, 3, 7]
CV_N1, CV_N2, CV_GB, CV_CDWB, CV_CLG, CV_CLB, CV_CDW, CV_FDW, CV_FDWB, CV_SINK = 0, 16, 32, 96, 100, 104, 108, 232, 496, 584
CV_L = 588


class Res:
    __slots__ = ("lw", "rd", "name")

    def __init__(self, name=""):
        self.lw = None
        self.rd = []
        self.name = name


class Op:
    __slots__ = ("eng", "fn", "deps", "needed", "sig", "isdma", "prev_on_sem")

    def __init__(self, eng, fn, isdma):
        self.eng = eng
        self.fn = fn
        self.deps = []
        self.needed = False
        self.sig = None
        self.isdma = isdma
        self.prev_on_sem = None


class Prog:
    ENGS = ("pe", "act", "dve", "pool", "sp")

    def __init__(self, n_dma_sems=40, sem_limit=30000):
        self.ops = {e: [] for e in self.ENGS}
        self.n_dma_sems = n_dma_sems
        self.sem_limit = sem_limit
        self.bar = None
        self.bar_seen = set()
        self.dma_since = []
        self.last_compute = {}

    def op(self, eng, fn, reads=(), writes=(), dma=False, nobar=False):
        o = Op(eng, fn, dma)
        deps = []
        for r in reads:
            if r.lw is not None:
                deps.append(r.lw)
        for w in writes:
            if w.lw is not None:
                deps.append(w.lw)
            deps.extend(w.rd)
        if self.bar is not None and eng not in self.bar_seen:
            deps.append(self.bar)
            self.bar_seen.add(eng)
        seen = set()
        for d in deps:
            if d is o or id(d) in seen:
                continue
            seen.add(id(d))
            if d.eng == "pe" and eng == "pe" and not d.isdma and not dma:
                continue
            o.deps.append(d)
            d.needed = True
        for r in reads:
            r.rd.append(o)
        for w in writes:
            w.lw = o
            w.rd = []
        self.ops[eng].append(o)
        if dma:
            if not nobar:
                self.dma_since.append(o)
        else:
            self.last_compute[eng] = o
        return o

    def barrier(self):
        o = Op("sp", lambda e: e.nop(nofuse=True), False)
        for e, lo in self.last_compute.items():
            if e == "sp":
                continue
            o.deps.append(lo)
            lo.needed = True
        for d in self.dma_since:
            o.deps.append(d)
        o.needed = True
        self.ops["sp"].append(o)
        self.dma_since = []
        self.last_compute = {"sp": o}
        self.bar = o
        self.bar_seen = {"sp"}
        return o

    def emit(self, nc, stack):
        engsems = {}
        for e in self.ENGS:
            cnt = sum(1 for o in self.ops[e] if o.needed and not o.isdma)
            n = max(1, -(-cnt // self.sem_limit))
            engsems[e] = [stack.enter_context(nc.semaphore(f"s_{e}{i}")) for i in range(n)]
        dsems = [stack.enter_context(nc.semaphore(f"s_dma{i}")) for i in range(self.n_dma_sems)]
        dcount = [0] * self.n_dma_sems
        dlast = [None] * self.n_dma_sems
        dma_engs = [e for e in self.ENGS if any(o.isdma for o in self.ops[e])]
        share = {}
        if dma_engs:
            tot = sum(sum(1 for o in self.ops[e] if o.isdma) for e in dma_engs)
            base = 0
            for i, e in enumerate(dma_engs):
                ne = sum(1 for o in self.ops[e] if o.isdma)
                k = max(4, int(round(self.n_dma_sems * ne / max(tot, 1))))
                if i == len(dma_engs) - 1:
                    k = self.n_dma_sems - base
                k = max(1, min(k, self.n_dma_sems - base - (len(dma_engs) - 1 - i) * 4))
                share[e] = (base, k)
                base += k
        for e in self.ENGS:
            c = 0
            rr = 0
            for o in self.ops[e]:
                if o.isdma:
                    b, k = share[e]
                    j = b + (rr % k)
                    rr += 1
                    dcount[j] += 16
                    o.sig = (dsems[j], dcount[j])
                    o.prev_on_sem = dlast[j]
                    dlast[j] = o
                elif o.needed:
                    o.sig = (engsems[e][c // self.sem_limit], (c % self.sem_limit) + 1)
                    c += 1
        block = stack.enter_context(nc.Block())
        handles = {"pe": block.tensor, "act": block.scalar, "dve": block.vector, "pool": block.gpsimd, "sp": block.sync}
        stats = {}
        for e in self.ENGS:
            ops = self.ops[e]

            def body(engine, ops=ops, e=e):
                seen = {}
                nw = 0
                for o in ops:
                    waits = {}
                    dl = o.deps
                    if o.prev_on_sem is not None:
                        dl = dl + [o.prev_on_sem]
                    for d in dl:
                        sem, val = d.sig
                        k = id(sem)
                        if seen.get(k, 0) >= val:
                            continue
                        if k not in waits or waits[k][1] < val:
                            waits[k] = (sem, val)
                    for k, (sem, val) in waits.items():
                        engine.wait_ge(sem, val)
                        seen[k] = val
                        nw += 1
                    ins = o.fn(engine)
                    if o.isdma:
                        ins.then_inc(o.sig[0], 16)
                    elif o.needed:
                        ins.then_inc(o.sig[0], 1)
                stats[e] = (len(ops), nw)

            handles[e](body)
        return stats


class Tile:
    _n = [0]

    def __init__(self, st, nc, shape, dtype, name="t"):
        Tile._n[0] += 1
        self.t = st.enter_context(nc.sbuf_tensor(f"{name}_{Tile._n[0]}", list(shape), dtype))
        self.r = Res(name)


class Ring:
    def __init__(self, st, nc, n, shape, dtype, name="r"):
        self.tiles = [Tile(st, nc, shape, dtype, name) for _ in range(n)]
        self.i = 0

    def next(self):
        t = self.tiles[self.i % len(self.tiles)]
        self.i += 1
        return t


class Rot:
    def __init__(self, items):
        self.items = list(items)
        self.i = 0

    def next(self):
        v = self.items[self.i % len(self.items)]
        self.i += 1
        return v


def build_nc(NSEQ=3, DEPTH=2, dbg=False, stop_after=None, sem_limit=30000):
    L = DEPTH
    nc = bass.Bass("TRN2", target_bir_lowering=False)
    dt_in = lambda n, s: nc.dram_tensor(n, list(s), F32, kind="ExternalInput").ap()
    xT = dt_in("xT", [NSEQ, D, T])
    win32 = dt_in("win", [L, NCW * 128, 2048])
    wbr32 = dt_in("wbr", [L, 64 * 128, 512])
    wout32 = dt_in("wout", [L, 16 * 128, 2048])
    wup32 = dt_in("wup", [L, 88 * 128, 2048])
    wdn32 = dt_in("wdn", [L, 16 * 128 * 4, 1408])
    cvec_d = dt_in("cvec", [128, CV_L * L + 16])
    gmrep_d = dt_in("gmrep", [L, 128, 6, 512])
    wsT_d = dt_in("wsT", [L, 128, 512])
    rope_d = dt_in("rope", [128, 2, T])
    swmask_d = dt_in("swmask", [128, 384])
    nag_d = dt_in("nag", [L, 4, 128, 1920])
    namask_d = dt_in("namask", [128, 1920])
    ident_d = dt_in("ident", [128, 128])
    yT = nc.dram_tensor("yT", [NSEQ, D, T], F32, kind="ExternalOutput").ap()
    dt_i = lambda n, s, d=BF16: nc.dram_tensor(n, list(s), d, kind="Internal").ap()
    win = dt_i("win_bf", [L, NCW * 128, 2048])
    wbr = dt_i("wbr_bf", [L, 64 * 128, 512])
    wout = dt_i("wout_bf", [L, 16 * 128, 2048])
    wup = dt_i("wup_bf", [L, 88 * 128, 2048])
    wdn = dt_i("wdn_bf", [L, 16 * 128 * 4, 1408])
    xa = dt_i("xa", [D, T], F32)
    xb = dt_i("xb", [D, T], F32)
    dbg_t = {}
    if dbg:
        dbg_t["hT"] = nc.dram_tensor("dbg_hT", [128, 16 * T], BF16, kind="ExternalOutput").ap()
        dbg_t["oall"] = nc.dram_tensor("dbg_oall", [128, 16 * T], BF16, kind="ExternalOutput").ap()
        dbg_t["xa"] = nc.dram_tensor("dbg_xa", [D, T], F32, kind="ExternalOutput").ap()
        dbg_t["xb"] = nc.dram_tensor("dbg_xb", [D, T], F32, kind="ExternalOutput").ap()

    P = Prog(sem_limit=sem_limit)
    with ExitStack() as st:
        hT = st.enter_context(nc.sbuf_tensor("hT", [128, 16, T], BF16))
        r_hT = [Res(f"hT{t}") for t in range(NT)]
        cvec = Tile(st, nc, [128, CV_L * L + 16], F32, "cvec")
        ones_bf = Tile(st, nc, [128, 128], BF16, "ones_bf")
        ones_f = Tile(st, nc, [128, 128], F32, "ones_f")
        ident = Tile(st, nc, [128, 128], F32, "ident")
        ps = st.enter_context(nc.psum_tensor("ps", [128, 8, 512], F32))
        r_ps = [Res(f"ps{b}") for b in range(8)]
        r_xa, r_xb = Res("xa"), Res("xb")
        out_ops = []

        def cv(col, n=1):
            return cvec.t[:, col:col + n]

        P.op("sp", lambda e: e.dma_start(out=cvec.t[:], in_=cvec_d), writes=[cvec.r], dma=True)
        P.op("dve", lambda e: e.memset(ones_bf.t[:], 1.0), writes=[ones_bf.r])
        P.op("dve", lambda e: e.memset(ones_f.t[:], 1.0), writes=[ones_f.r])
        P.op("sp", lambda e: e.dma_start(out=ident.t[:], in_=ident_d), writes=[ident.r], dma=True)

        wst_all = Tile(st, nc, [128, L, 512], BF16, "gws")
        for l_ in range(L):
            P.op("pool", lambda e, l_=l_: e.dma_start(out=wst_all.t[:, l_, :], in_=wsT_d[l_]), writes=[wst_all.r], dma=True)

        wres = {}

        def cast(name, l, dst, src, r0, r1, keys):
            r = Res(f"{name}{l}_{r0}")
            P.op("pool", lambda e: e.dma_start(out=dst[l, r0:r1, :], in_=src[l, r0:r1, :]), writes=[r], dma=True, nobar=True)
            for k in keys:
                wres[(name, l, k)] = r

        MIX_ORDER = [4, 5, 6, 7, 0, 1, 2, 3, 16, 17, 18, 8, 12, 9, 13, 10, 14, 11, 15, 19, 23, 20, 24, 21, 25, 22, 26,
                     27, 31, 35, 28, 32, 36, 29, 33, 37, 30, 34, 38]

        def issue_casts(l, part):
            if part == 0:
                for c in MIX_ORDER[:8]:
                    cast("win", l, win, win32, c * 128, (c + 1) * 128, [c])
                return
            for c in MIX_ORDER[8:]:
                cast("win", l, win, win32, c * 128, (c + 1) * 128, [c])
            for c0 in range(CH_G, NCW, 8):
                c1 = min(NCW, c0 + 8)
                cast("win", l, win, win32, c0 * 128, c1 * 128, range(c0, c1))
            for b in range(4):
                cast("wbr", l, wbr, wbr32, b * 2048, (b + 1) * 2048, [(b, d) for d in range(16)])
            for d0 in range(0, 16, 8):
                cast("wout", l, wout, wout32, d0 * 128, (d0 + 8) * 128, range(d0, d0 + 8))
            for c0 in range(0, 88, 8):
                cast("wup", l, wup, wup32, c0 * 128, (c0 + 8) * 128, range(c0, c0 + 8))
            for d0 in range(0, 16, 2):
                cast("wdn", l, wdn, wdn32, d0 * 512, (d0 + 2) * 512, range(d0, d0 + 2))

        issue_casts(0, 0)

        def wl_win(tile, l, c, dst=None):
            dst = tile.t[:] if dst is None else dst
            P.op("sp", lambda e: e.dma_start(out=dst, in_=win[l, c * 128:(c + 1) * 128, :].rearrange("p (k j) -> p k j", k=16)),
                 reads=[wres[("win", l, c)]], writes=[tile.r], dma=True)

        def tok(tt):
            return slice(tt * TT, (tt + 1) * TT)

        def proj_fm(wt, banks, extra_reads=()):
            for k in range(16):
                for tt in range(NT):
                    P.op("pe", lambda e, k=k, tt=tt: e.matmul(ps[:, banks[tt], :], lhsT=wt.t[:, k, :], rhs=hT[:, k, tok(tt)],
                                                               start=(k == 0), stop=(k == 15)),
                         reads=[wt.r, r_hT[tt]], writes=[r_ps[banks[tt]]])

        def proj_tm(dst_cols, wt_ap_fn, wt_res, t0, bank, ncols):
            for k in range(16):
                P.op("pe", lambda e, k=k: e.matmul(ps[:, bank, dst_cols], lhsT=hT[:, k, t0:t0 + 128], rhs=wt_ap_fn(k),
                                                   start=(k == 0), stop=(k == 15)),
                     reads=[wt_res] + [r_hT[tt] for tt in sorted({t0 // TT, (t0 + 127) // TT})], writes=[r_ps[bank]])

        def phase_norm(src, src_res, gcol, mode, dst=None, dst_res=None):
            with ExitStack() as ph:
                xr = Ring(ph, nc, 2, [128, 16, TT], F32, "nx")
                sqr = Ring(ph, nc, 3, [128, TT], BF16, "nsq")
                rtr = Ring(ph, nc, 2, [128, TT], F32, "nrt")
                rsr = Ring(ph, nc, 2, [128, TT], F32, "nrs")
                yor = Ring(ph, nc, 4, [128, TT], F32, "nyo")
                bank = Rot([0, 1])
                srcv = src.rearrange("(k p) t -> p k t", p=128)
                xts = {}

                def load_x(tt):
                    xt = xr.next()
                    for h in range(2):
                        P.op("sp", lambda e, xt=xt, tt=tt, h=h: e.dma_start(out=xt.t[:, h * 8:(h + 1) * 8, :], in_=srcv[:, h * 8:(h + 1) * 8, tok(tt)]),
                             reads=[src_res], writes=[xt.r], dma=True)
                    xts[tt] = xt

                load_x(0)
                for tt in range(NT):
                    if tt + 1 < NT:
                        load_x(tt + 1)
                    xt = xts.pop(tt)
                    b = bank.next()
                    for k in range(16):
                        sq = sqr.next()
                        P.op("act", lambda e, xt=xt, sq=sq, k=k: e.activation(out=sq.t[:], in_=xt.t[:, k, :], func=AF.Square),
                             reads=[xt.r], writes=[sq.r])
                        P.op("pe", lambda e, sq=sq, k=k, b=b: e.matmul(ps[:, b, :], lhsT=ones_bf.t[:], rhs=sq.t[:], start=(k == 0), stop=(k == 15)),
                             reads=[sq.r, ones_bf.r], writes=[r_ps[b]])
                    rt = rtr.next()
                    rs = rsr.next()
                    P.op("act", lambda e, rt=rt, b=b: e.activation(out=rt.t[:], in_=ps[:, b, :], func=AF.Sqrt, bias=EPS, scale=1.0 / D),
                         reads=[r_ps[b]], writes=[rt.r])
                    P.op("dve", lambda e, rt=rt, rs=rs: e.reciprocal(out=rs.t[:], in_=rt.t[:]), reads=[rt.r], writes=[rs.r])
                    for k in range(16):
                        if mode == "h":
                            P.op("dve", lambda e, xt=xt, rs=rs, k=k, tt=tt: e.scalar_tensor_tensor(
                                out=hT[:, k, tok(tt)], in0=xt.t[:, k, :], scalar=cv(gcol + k), in1=rs.t[:], op0=ALU.mult, op1=ALU.mult),
                                reads=[xt.r, rs.r, cvec.r], writes=[r_hT[tt]])
                        else:
                            yo = yor.next()
                            P.op("dve", lambda e, xt=xt, rs=rs, k=k, yo=yo: e.scalar_tensor_tensor(
                                out=yo.t[:], in0=xt.t[:, k, :], scalar=cv(gcol + k), in1=rs.t[:], op0=ALU.mult, op1=ALU.mult),
                                reads=[xt.r, rs.r, cvec.r], writes=[yo.r])
                            o = P.op("sp", lambda e, yo=yo, k=k, tt=tt: e.dma_start(out=dst[k * 128:(k + 1) * 128, tok(tt)], in_=yo.t[:]),
                                     reads=[yo.r], writes=[dst_res], dma=True)
                            out_ops.append(o)
                P.barrier()

        def mixer_gm(l, oall, r_o):
            with ExitStack() as ph:
                wv = Tile(ph, nc, [128, 4, 16, 128], BF16, "gwv")
                r_wv = [Res(f"wv{c}") for c in range(4)]
                wur = Ring(ph, nc, 2, [128, 16, 128], BF16, "gwu")
                rep = Tile(ph, nc, [128, 6, 512], F32, "grep")
                vtok = Tile(ph, nc, [128, 16, 512], BF16, "gvt")
                r_vt = [Res(f"vt{n}") for n in range(16)]
                st6 = Ring(ph, nc, 3, [128, 6], F32, "gst")
                mvr = Ring(ph, nc, 3, [128, 2], F32, "gmv")
                sdr = Ring(ph, nc, 3, [128, 1], F32, "gsd")
                rsr = Ring(ph, nc, 3, [128, 1], F32, "grs")
                nmr = Ring(ph, nc, 3, [128, 1], F32, "gnm")
                vtr = Ring(ph, nc, 2, [128, 512], F32, "gvtmp")
                tar = Ring(ph, nc, 2, [128, 512], F32, "gta")
                P.op("sp", lambda e: e.dma_start(out=rep.t[:], in_=gmrep_d[l]), writes=[rep.r], dma=True)
                for c in range(4):
                    P.op("sp", lambda e, c=c: e.dma_start(out=wv.t[:, c, :, :],
                                                           in_=win[l, (CH_V + c) * 128:(CH_V + c + 1) * 128, :].rearrange("p (k j) -> p k j", k=16)),
                         reads=[wres[("win", l, CH_V + c)]], writes=[r_wv[c]], dma=True)
                vb = Rot([0, 1, 2, 3])
                for n in range(16):
                    b = vb.next()
                    for k in range(16):
                        P.op("pe", lambda e, k=k, n=n, b=b: e.matmul(ps[:, b, :].rearrange("p (c j) -> p c j", c=4), lhsT=hT[:, k, n * 128:(n + 1) * 128], rhs=wv.t[:, :, k, :],
                                                                      start=(k == 0), stop=(k == 15)),
                             reads=r_wv + [r_hT[n // 4]], writes=[r_ps[b]])
                    s6, mv, sd, rs, nm, vt = st6.next(), mvr.next(), sdr.next(), rsr.next(), nmr.next(), vtr.next()
                    P.op("dve", lambda e, s6=s6, b=b: e.bn_stats(out=s6.t[:], in_=ps[:, b, :]), reads=[r_ps[b]], writes=[s6.r])
                    P.op("dve", lambda e, s6=s6, mv=mv: e.bn_aggr(out=mv.t[:], in_=s6.t[:]), reads=[s6.r], writes=[mv.r])
                    P.op("act", lambda e, mv=mv, sd=sd: e.activation(out=sd.t[:], in_=mv.t[:, 1:2], func=AF.Sqrt, bias=EPS, scale=1.0), reads=[mv.r], writes=[sd.r])
                    P.op("dve", lambda e, sd=sd, rs=rs: e.reciprocal(out=rs.t[:], in_=sd.t[:]), reads=[sd.r], writes=[rs.r])
                    P.op("dve", lambda e, mv=mv, rs=rs, nm=nm: e.scalar_tensor_tensor(out=nm.t[:], in0=mv.t[:, 0:1], scalar=-1.0, in1=rs.t[:], op0=ALU.mult, op1=ALU.mult),
                         reads=[mv.r, rs.r], writes=[nm.r])
                    P.op("act", lambda e, vt=vt, b=b, rs=rs, nm=nm: e.activation(out=vt.t[:], in_=ps[:, b, :], func=AF.Identity, bias=nm.t[:], scale=rs.t[:]),
                         reads=[r_ps[b], rs.r, nm.r], writes=[vt.r])
                    P.op("dve", lambda e, vt=vt: e.tensor_tensor(out=vt.t[:], in0=vt.t[:], in1=rep.t[:, 0, :], op=ALU.mult), reads=[vt.r, rep.r], writes=[vt.r])
                    P.op("dve", lambda e, vt=vt, n=n: e.tensor_tensor(out=vtok.t[:, n, :], in0=vt.t[:], in1=rep.t[:, 1, :], op=ALU.add), reads=[vt.r, rep.r], writes=[r_vt[n]])
                mb = Rot([4, 5])
                for g in range(4):
                    wu = wur.next()
                    wl_win(wu, l, CH_U + g)
                    proj_fm(wu, [0, 1, 2, 3])
                    for tt in range(NT):
                        m = mb.next()
                        for nn in range(4):
                            n = tt * 4 + nn
                            P.op("pe", lambda e, m=m, nn=nn, n=n, g=g: e.matmul(ps[:, m, nn * 128:(nn + 1) * 128], lhsT=vtok.t[:, n, g * 128:(g + 1) * 128],
                                                                                 rhs=wst_all.t[:, l, g * 128:(g + 1) * 128], start=True, stop=True),
                                 reads=[r_vt[n], wst_all.r], writes=[r_ps[m]])
                        ta = tar.next()
                        P.op("dve", lambda e, ta=ta, m=m, g=g: e.tensor_tensor(out=ta.t[:], in0=ps[:, m, :], in1=rep.t[:, 2 + g, :], op=ALU.add),
                             reads=[r_ps[m], rep.r], writes=[ta.r])
                        P.op("dve", lambda e, ta=ta, g=g, tt=tt: e.tensor_tensor(out=oall[:, g, tok(tt)], in0=ps[:, tt, :], in1=ta.t[:], op=ALU.mult),
                             reads=[r_ps[tt], ta.r], writes=[r_o[g][tt]])
                P.barrier()

        def mixer_sw(l, oall, r_o):
            with ExitStack() as ph:
                rope = Tile(ph, nc, [128, 2, T], F32, "rope")
                mask = Tile(ph, nc, [128, 384], F32, "swm")
                esk = Tile(ph, nc, [128, 4], F32, "esk")
                wr = Ring(ph, nc, 3, [128, 16, 128], BF16, "sww")
                kr = Tile(ph, nc, [128, T], BF16, "kr")
                vt = Tile(ph, nc, [128, 16, 128], BF16, "swv")
                qrr = Ring(ph, nc, 2, [128, T], BF16, "qr")
                t1r = Ring(ph, nc, 2, [128, TT], F32, "t1")
                t2r = Ring(ph, nc, 2, [128, TT], F32, "t2")
                smr = Ring(ph, nc, 2, [128, 384], F32, "sm")
                ptr = Ring(ph, nc, 3, [128, 384], BF16, "pt")
                dnr = Ring(ph, nc, 2, [128, TT], F32, "dn")
                P.op("sp", lambda e: e.dma_start(out=rope.t[:], in_=rope_d), writes=[rope.r], dma=True)
                P.op("sp", lambda e: e.dma_start(out=mask.t[:], in_=swmask_d), writes=[mask.r], dma=True)
                P.op("act", lambda e: e.activation(out=esk.t[:], in_=cv(l * CV_L + CV_SINK, 4), func=AF.Exp), reads=[cvec.r], writes=[esk.r])

                def rope_evac(dst, dst_res):
                    for tt in range(NT):
                        t1, t2 = t1r.next(), t2r.next()
                        P.op("dve", lambda e, t1=t1, tt=tt: e.tensor_tensor(out=t1.t[:], in0=ps[:, tt, :], in1=rope.t[:, 0, tok(tt)], op=ALU.mult),
                             reads=[r_ps[tt], rope.r], writes=[t1.r])
                        P.op("dve", lambda e, t2=t2, tt=tt: e.tensor_tensor(out=t2.t[:], in0=ps[:, 4 + tt, :], in1=rope.t[:, 1, tok(tt)], op=ALU.mult),
                             reads=[r_ps[4 + tt], rope.r], writes=[t2.r])
                        P.op("dve", lambda e, t1=t1, t2=t2, tt=tt: e.tensor_tensor(out=dst[:, tok(tt)], in0=t1.t[:], in1=t2.t[:], op=ALU.add),
                             reads=[t1.r, t2.r], writes=[dst_res])

                wk, wkp, wsv = wr.next(), wr.next(), wr.next()
                wl_win(wk, l, CH_K)
                wl_win(wkp, l, CH_KP)
                wl_win(wsv, l, CH_SV)
                proj_fm(wk, [0, 1, 2, 3])
                proj_fm(wkp, [4, 5, 6, 7])
                rope_evac(kr.t, kr.r)
                vb = Rot([0, 1])
                for nb4 in range(4):
                    b = vb.next()
                    for nn in range(4):
                        n = nb4 * 4 + nn
                        proj_tm(slice(nn * 128, (nn + 1) * 128), lambda k: wsv.t[:, k, :], wsv.r, n * 128, b, 128)
                    P.op("act", lambda e, b=b, nb4=nb4: e.activation(out=vt.t[:, nb4 * 4:(nb4 + 1) * 4, :].rearrange("p a b -> p (a b)"), in_=ps[:, b, :], func=AF.Copy),
                         reads=[r_ps[b]], writes=[vt.r])
                for j in range(4):
                    wq, wqp = wr.next(), wr.next()
                    wl_win(wq, l, CH_Q + j)
                    wl_win(wqp, l, CH_QP + j)
                    proj_fm(wq, [0, 1, 2, 3])
                    proj_fm(wqp, [4, 5, 6, 7])
                    qr = qrr.next()
                    rope_evac(qr.t, qr.r)
                    sb = Rot([0, 1, 2, 3])

                    def sw_qk(qt, nn, hh, qr=qr):
                        n = qt * 4 + nn
                        S = sb.next()
                        hs = slice(hh * 64, (hh + 1) * 64)
                        slots = [sl for sl in range(3) if 0 <= n - 1 + sl < 16]
                        for sl in slots:
                            kb = n - 1 + sl
                            P.op("pe", lambda e, S=S, sl=sl, kb=kb, hs=hs, qr=qr, n=n: e.matmul(
                                ps[:, S, sl * 128:(sl + 1) * 128], lhsT=kr.t[hs, kb * 128:(kb + 1) * 128], rhs=qr.t[hs, n * 128:(n + 1) * 128], start=True, stop=True),
                                reads=[kr.r, qr.r], writes=[r_ps[S]])
                        lo, hi = slots[0] * 128, (slots[-1] + 1) * 128
                        sm, pt = smr.next(), ptr.next()
                        P.op("dve", lambda e, sm=sm, S=S, lo=lo, hi=hi: e.scalar_tensor_tensor(
                            out=sm.t[:, lo:hi], in0=ps[:, S, lo:hi], scalar=0.125, in1=mask.t[:, lo:hi], op0=ALU.mult, op1=ALU.add),
                            reads=[r_ps[S], mask.r], writes=[sm.r])
                        P.op("act", lambda e, sm=sm, pt=pt, lo=lo, hi=hi: e.activation(out=pt.t[:, lo:hi], in_=sm.t[:, lo:hi], func=AF.Exp),
                             reads=[sm.r], writes=[pt.r])
                        return (qt, nn, hh, n, hs, slots, pt)

                    def sw_pv(state, j=j):
                        qt, nn, hh, n, hs, slots, pt = state
                        Ob, Db = (4, 5) if qt % 2 == 0 else (6, 7)
                        for sl in slots:
                            kb = n - 1 + sl
                            P.op("pe", lambda e, Ob=Ob, nn=nn, sl=sl, kb=kb, hs=hs, pt=pt, slots=slots: e.matmul(
                                ps[hs, Ob, nn * 128:(nn + 1) * 128], lhsT=vt.t[:, kb, hs], rhs=pt.t[:, sl * 128:(sl + 1) * 128],
                                start=(sl == slots[0]), stop=(sl == slots[-1])),
                                reads=[vt.r, pt.r], writes=[r_ps[Ob]])
                        for sl in slots:
                            P.op("pe", lambda e, Db=Db, nn=nn, sl=sl, hs=hs, pt=pt, slots=slots: e.matmul(
                                ps[hs, Db, nn * 128:(nn + 1) * 128], lhsT=ones_bf.t[:, 0:64], rhs=pt.t[:, sl * 128:(sl + 1) * 128],
                                start=(sl == slots[0]), stop=(sl == slots[-1])),
                                reads=[ones_bf.r, pt.r], writes=[r_ps[Db]])
                        if nn == 3 and hh == 1:
                            dn = dnr.next()
                            P.op("dve", lambda e, dn=dn, Db=Db, j=j: e.tensor_scalar(out=dn.t[:], in0=ps[:, Db, :], scalar1=esk.t[:, j:j + 1], scalar2=None, op0=ALU.add),
                                 reads=[r_ps[Db], esk.r], writes=[dn.r])
                            P.op("dve", lambda e, dn=dn: e.reciprocal(out=dn.t[:], in_=dn.t[:]), reads=[dn.r], writes=[dn.r])
                            P.op("dve", lambda e, dn=dn, Ob=Ob, j=j, qt=qt: e.tensor_tensor(out=oall[:, 4 + j, tok(qt)], in0=ps[:, Ob, :], in1=dn.t[:], op=ALU.mult),
                                 reads=[r_ps[Ob], dn.r], writes=[r_o[4 + j][qt]])

                    pend = []
                    for qt in range(4):
                        for nn in range(4):
                            for hh in range(2):
                                pend.append(sw_qk(qt, nn, hh))
                                if len(pend) > 2:
                                    sw_pv(pend.pop(0))
                    while pend:
                        sw_pv(pend.pop(0))
                P.barrier()

        def mixer_cv(l, oall, r_o):
            with ExitStack() as ph:
                wr = Ring(ph, nc, 3, [128, 16, 128], BF16, "cvw")
                xpr = Ring(ph, nc, 1, [128, T + 30], BF16, "xpad")
                dgr = Ring(ph, nc, 1, [128, 31, 128], BF16, "cvdg")
                sgr = Ring(ph, nc, 2, [128, TT], F32, "cvsg")
                y = Tile(ph, nc, [128, 4, T], F32, "cvy")
                r_y = [[Res(f"y{c}_{tt}") for tt in range(NT)] for c in range(4)]
                sqr = Ring(ph, nc, 1, [128, TT], F32, "cvsq")
                mer = Ring(ph, nc, 1, [128, TT], F32, "cvme")
                msr = Ring(ph, nc, 1, [128, TT], F32, "cvms")
                rsr = Ring(ph, nc, 1, [128, TT], F32, "cvrs")
                tmr = Ring(ph, nc, 2, [128, TT], F32, "cvtm")
                for xp in xpr.tiles:
                    P.op("dve", lambda e, xp=xp: e.memset(xp.t[:, 0:15], 0.0), writes=[xp.r])
                    P.op("dve", lambda e, xp=xp: e.memset(xp.t[:, T + 15:T + 30], 0.0), writes=[xp.r])
                base = l * CV_L
                for c in range(4):
                    wa, wg = wr.next(), wr.next()
                    wl_win(wa, l, CH_CA + c)
                    wl_win(wg, l, CH_CG + c)
                    dg = dgr.next()
                    for k in range(31):
                        P.op("dve", lambda e, dg=dg, k=k, c=c: e.tensor_scalar(out=dg.t[:, k, :], in0=ident.t[:], scalar1=cv(base + CV_CDW + k * 4 + c), scalar2=None, op0=ALU.mult),
                             reads=[ident.r, cvec.r], writes=[dg.r])
                    proj_fm(wa, [0, 1, 2, 3])
                    proj_fm(wg, [4, 5, 6, 7])
                    xp = xpr.next()
                    for tt in range(NT):
                        sl = slice(15 + tt * TT, 15 + (tt + 1) * TT)
                        sg = sgr.next()
                        P.op("act", lambda e, sg=sg, tt=tt: e.activation(out=sg.t[:], in_=ps[:, 4 + tt, :], func=AF.Sigmoid),
                             reads=[r_ps[4 + tt]], writes=[sg.r])
                        P.op("dve", lambda e, xp=xp, sl=sl, tt=tt, sg=sg: e.tensor_tensor(out=xp.t[:, sl], in0=ps[:, tt, :], in1=sg.t[:], op=ALU.mult),
                             reads=[r_ps[tt], sg.r], writes=[xp.r])
                    for tt in range(NT):
                        for k in range(31):
                            P.op("pe", lambda e, dg=dg, xp=xp, k=k, tt=tt: e.matmul(ps[:, 4 + tt, :], lhsT=dg.t[:, k, :], rhs=xp.t[:, tt * TT + k:tt * TT + k + TT],
                                                                                 start=(k == 0), stop=(k == 30)),
                                 reads=[dg.r, xp.r], writes=[r_ps[4 + tt]])
                        P.op("act", lambda e, c=c, tt=tt: e.activation(out=y.t[:, c, tok(tt)], in_=ps[:, 4 + tt, :], func=AF.Identity, bias=cv(base + CV_CDWB + c), scale=1.0),
                             reads=[r_ps[4 + tt], cvec.r], writes=[r_y[c][tt]])
                sbk = Rot([(0, 1), (2, 3)])
                for tt in range(NT):
                    S1, S2 = sbk.next()
                    for c in range(4):
                        sq = sqr.next()
                        P.op("act", lambda e, sq=sq, c=c, tt=tt: e.activation(out=sq.t[:], in_=y.t[:, c, tok(tt)], func=AF.Square), reads=[r_y[c][tt]], writes=[sq.r])
                        P.op("pe", lambda e, c=c, tt=tt, S1=S1: e.matmul(ps[:, S1, :], lhsT=ones_f.t[:], rhs=y.t[:, c, tok(tt)], start=(c == 0), stop=(c == 3)),
                             reads=[r_y[c][tt], ones_f.r], writes=[r_ps[S1]])
                        P.op("pe", lambda e, sq=sq, c=c, S2=S2: e.matmul(ps[:, S2, :], lhsT=ones_f.t[:], rhs=sq.t[:], start=(c == 0), stop=(c == 3)),
                             reads=[sq.r, ones_f.r], writes=[r_ps[S2]])
                    me, ms, rs = mer.next(), msr.next(), rsr.next()
                    P.op("dve", lambda e, me=me, S1=S1: e.tensor_scalar(out=me.t[:], in0=ps[:, S1, :], scalar1=1.0 / 512, scalar2=None, op0=ALU.mult), reads=[r_ps[S1]], writes=[me.r])
                    P.op("dve", lambda e, me=me, ms=ms: e.tensor_tensor(out=ms.t[:], in0=me.t[:], in1=me.t[:], op=ALU.mult), reads=[me.r], writes=[ms.r])
                    P.op("dve", lambda e, ms=ms, S2=S2: e.scalar_tensor_tensor(out=ms.t[:], in0=ps[:, S2, :], scalar=1.0 / 512, in1=ms.t[:], op0=ALU.mult, op1=ALU.subtract),
                         reads=[r_ps[S2], ms.r], writes=[ms.r])
                    P.op("act", lambda e, ms=ms, rs=rs: e.activation(out=rs.t[:], in_=ms.t[:], func=AF.Sqrt, bias=EPS, scale=1.0), reads=[ms.r], writes=[rs.r])
                    P.op("dve", lambda e, rs=rs: e.reciprocal(out=rs.t[:], in_=rs.t[:]), reads=[rs.r], writes=[rs.r])
                    for c in range(4):
                        tm = tmr.next()
                        P.op("dve", lambda e, tm=tm, c=c, tt=tt, me=me: e.tensor_tensor(out=tm.t[:], in0=y.t[:, c, tok(tt)], in1=me.t[:], op=ALU.subtract),
                             reads=[r_y[c][tt], me.r], writes=[tm.r])
                        P.op("dve", lambda e, tm=tm, rs=rs: e.tensor_tensor(out=tm.t[:], in0=tm.t[:], in1=rs.t[:], op=ALU.mult), reads=[tm.r, rs.r], writes=[tm.r])
                        P.op("act", lambda e, tm=tm, c=c, tt=tt: e.activation(out=oall[:, 8 + c, tok(tt)], in_=tm.t[:], func=AF.Silu,
                                                                             bias=cv(base + CV_CLB + c), scale=cv(base + CV_CLG + c)),
                             reads=[tm.r, cvec.r], writes=[r_o[8 + c][tt]])
                P.barrier()

        def mixer_na(l, oall, r_o):
            with ExitStack() as ph:
                wr = Ring(ph, nc, 3, [128, 16, 128], BF16, "naw")
                mk = Tile(ph, nc, [128, 2, 15, 64], F32, "namk")
                gtr = Ring(ph, nc, 2, [128, 2, 15, 64], F32, "nagt")
                qTr = Ring(ph, nc, 2, [128, T], BF16, "naq")
                kTr = Ring(ph, nc, 2, [128, T], BF16, "nak")
                ver = Ring(ph, nc, 1, [128, 16, 128], BF16, "nave")
                vor = Ring(ph, nc, 1, [128, 16, 128], BF16, "navo")
                smr = Ring(ph, nc, 2, [128, 2, 4, 64], F32, "nasm")
                ptr = Ring(ph, nc, 3, [128, 512], BF16, "napt")
                dnr = Ring(ph, nc, 2, [128, TT], F32, "nadn")
                P.op("sp", lambda e: e.dma_start(out=mk.t[:].rearrange("p a b c -> p (a b c)"), in_=namask_d), writes=[mk.r], dma=True)
                for j in range(4):
                    wq, wk, wv = wr.next(), wr.next(), wr.next()
                    wl_win(wq, l, CH_NQ + j)
                    wl_win(wk, l, CH_NK + j)
                    wl_win(wv, l, CH_NV + j)
                    bt = gtr.next()
                    P.op("sp", lambda e, bt=bt, j=j: e.dma_start(out=bt.t[:].rearrange("p a b c -> p (a b c)"), in_=nag_d[l, j]), writes=[bt.r], dma=True)
                    P.op("dve", lambda e, bt=bt: e.tensor_tensor(out=bt.t[:].rearrange("p a b c -> p (a b c)"), in0=bt.t[:].rearrange("p a b c -> p (a b c)"),
                                                                  in1=mk.t[:].rearrange("p a b c -> p (a b c)"), op=ALU.add),
                         reads=[bt.r, mk.r], writes=[bt.r])
                    qT, kT, ve, vo = qTr.next(), kTr.next(), ver.next(), vor.next()
                    proj_fm(wq, [0, 1, 2, 3])
                    proj_fm(wk, [4, 5, 6, 7])
                    for tt in range(NT):
                        P.op("act", lambda e, qT=qT, tt=tt: e.activation(out=qT.t[:, tok(tt)], in_=ps[:, tt, :], func=AF.Identity, scale=0.125),
                             reads=[r_ps[tt]], writes=[qT.r])
                        P.op("dve", lambda e, kT=kT, tt=tt: e.tensor_copy(out=kT.t[:, tok(tt)], in_=ps[:, 4 + tt, :]), reads=[r_ps[4 + tt]], writes=[kT.r])
                    vb = Rot([0, 1, 2, 3])
                    for (vtile, off, nblk) in ((ve, 0, 16), (vo, 64, 15)):
                        for n0 in range(0, nblk, 4):
                            b = vb.next()
                            cnt = min(4, nblk - n0)
                            for nn in range(cnt):
                                n = n0 + nn
                                proj_tm(slice(nn * 128, (nn + 1) * 128), lambda k, wv=wv: wv.t[:, k, :], wv.r, off + n * 128, b, 128)
                            eng = "act" if (n0 // 4) % 2 == 0 else "dve"
                            if eng == "act":
                                P.op("act", lambda e, vtile=vtile, n0=n0, cnt=cnt, b=b: e.activation(
                                    out=vtile.t[:, n0:n0 + cnt, :].rearrange("p a b -> p (a b)"), in_=ps[:, b, 0:cnt * 128], func=AF.Copy),
                                    reads=[r_ps[b]], writes=[vtile.r])
                            else:
                                P.op("dve", lambda e, vtile=vtile, n0=n0, cnt=cnt, b=b: e.tensor_copy(
                                    out=vtile.t[:, n0:n0 + cnt, :].rearrange("p a b -> p (a b)"), in_=ps[:, b, 0:cnt * 128]),
                                    reads=[r_ps[b]], writes=[vtile.r])
                    sb = Rot([0, 2])

                    def na_qk(r, qT=qT, kT=kT, bt=bt):
                        rs_ = min(max(r - 4, 0), 24)
                        i0 = rs_ - r + 7
                        S = sb.next()
                        for hh in range(2):
                            hs = slice(hh * 64, (hh + 1) * 64)
                            for jj in range(4):
                                c0 = jj * 64
                                k0 = 64 * rs_ + 128 * jj
                                P.op("pe", lambda e, S=S, hh=hh, c0=c0, k0=k0, hs=hs, r=r: e.matmul(
                                    ps[:, S + hh, c0:c0 + 64], lhsT=kT.t[hs, k0:k0 + 128], rhs=qT.t[hs, 64 * r:64 * r + 64], start=True, stop=True),
                                    reads=[kT.r, qT.r], writes=[r_ps[S + hh]])
                        sm, pt = smr.next(), ptr.next()
                        P.op("dve", lambda e, sm=sm, S=S, i0=i0: e.tensor_tensor(
                            out=sm.t[:], in0=ps[:, S:S + 2, 0:256].rearrange("p a (b c) -> p a b c", b=4), in1=bt.t[:, :, i0:i0 + 8:2, :], op=ALU.add),
                            reads=[r_ps[S], r_ps[S + 1], bt.r], writes=[sm.r])
                        P.op("act", lambda e, sm=sm, pt=pt: e.activation(out=pt.t[:], in_=sm.t[:].rearrange("p a b c -> p (a b c)"), func=AF.Exp),
                             reads=[sm.r], writes=[pt.r])
                        return (r, rs_, pt)

                    def na_pv(state, ve=ve, vo=vo, j=j):
                        r, rs_, pt = state
                        vtile = ve if rs_ % 2 == 0 else vo
                        kb0 = rs_ // 2
                        slot = r % 8
                        Ob, Db = (4, 5) if (r // 8) % 2 == 0 else (6, 7)
                        for hh in range(2):
                            hs = slice(hh * 64, (hh + 1) * 64)
                            for jj in range(4):
                                c0 = (hh * 4 + jj) * 64
                                P.op("pe", lambda e, Ob=Ob, slot=slot, hs=hs, jj=jj, c0=c0, vtile=vtile, kb0=kb0, pt=pt: e.matmul(
                                    ps[hs, Ob, slot * 64:(slot + 1) * 64], lhsT=vtile.t[:, kb0 + jj, hs], rhs=pt.t[:, c0:c0 + 64], start=(jj == 0), stop=(jj == 3)),
                                    reads=[vtile.r, pt.r], writes=[r_ps[Ob]])
                            for jj in range(4):
                                c0 = (hh * 4 + jj) * 64
                                P.op("pe", lambda e, Db=Db, slot=slot, hs=hs, jj=jj, c0=c0, pt=pt: e.matmul(
                                    ps[hs, Db, slot * 64:(slot + 1) * 64], lhsT=ones_bf.t[:, 0:64], rhs=pt.t[:, c0:c0 + 64], start=(jj == 0), stop=(jj == 3)),
                                    reads=[ones_bf.r, pt.r], writes=[r_ps[Db]])
                        if slot == 7:
                            qt = r // 8
                            dn = dnr.next()
                            P.op("dve", lambda e, dn=dn, Db=Db: e.reciprocal(out=dn.t[:], in_=ps[:, Db, :]), reads=[r_ps[Db]], writes=[dn.r])
                            P.op("dve", lambda e, dn=dn, Ob=Ob, j=j, qt=qt: e.tensor_tensor(out=oall[:, 12 + j, tok(qt)], in0=ps[:, Ob, :], in1=dn.t[:], op=ALU.mult),
                                 reads=[r_ps[Ob], dn.r], writes=[r_o[12 + j][qt]])

                    pend = []
                    for r in range(32):
                        pend.append(na_qk(r))
                        if len(pend) > 1:
                            na_pv(pend.pop(0))
                    while pend:
                        na_pv(pend.pop(0))
                P.barrier()

        def phase_merge(l, oall, r_o, xsrc, xsrc_res):
            with ExitStack() as ph:
                merged = Tile(ph, nc, [128, 16, TT], BF16, "mrg")
                r_m = [Res(f"m{d}") for d in range(16)]
                wgr = Ring(ph, nc, 5, [128, 16, 128], BF16, "mwg")
                wbrr = Ring(ph, nc, 5, [128, 4, 128], BF16, "mwb")
                wor = Ring(ph, nc, 3, [128, 16, 128], BF16, "mwo")
                sgr = Ring(ph, nc, 2, [128, TT], F32, "msg")
                acr = Ring(ph, nc, 2, [128, TT], F32, "mac")
                tmr = Ring(ph, nc, 1, [128, TT], F32, "mtm")
                xir = Ring(ph, nc, 2, [128, TT], F32, "mxi")
                xor_ = Ring(ph, nc, 2, [128, TT], F32, "mxo")
                base = l * CV_L
                gb = Rot([0, 1, 2, 3])
                pb = Rot([4, 5, 6, 7])
                units = [(tt, d, b) for tt in range(NT) for d in range(16) for b in range(4)]
                loaded = {}

                def load_unit(u):
                    tt, d, b = u
                    wg, wb = wgr.next(), wbrr.next()
                    wl_win(wg, l, CH_G + b * 16 + d)
                    P.op("sp", lambda e, wb=wb, b=b, d=d: e.dma_start(out=wb.t[:], in_=wbr[l, (b * 16 + d) * 128:(b * 16 + d + 1) * 128, :].rearrange("p (k j) -> p k j", k=4)),
                         reads=[wres[("wbr", l, (b, d))]], writes=[wb.r], dma=True)
                    loaded[u] = (wg, wb)

                PF = 4
                for i in range(min(PF, len(units))):
                    load_unit(units[i])
                acc = None
                wo_loaded = {}

                def load_wo(dd):
                    wo = wor.next()
                    P.op("sp", lambda e, wo=wo, dd=dd: e.dma_start(out=wo.t[:], in_=wout[l, dd * 128:(dd + 1) * 128, :].rearrange("p (k j) -> p k j", k=16)),
                         reads=[wres[("wout", l, dd)]], writes=[wo.r], dma=True)
                    wo_loaded[dd] = wo

                for ui, u in enumerate(units):
                    if ui + PF < len(units):
                        load_unit(units[ui + PF])
                    tt, d, b = u
                    if d == 14 and b == 0:
                        for dd in range(2):
                            load_wo(dd)
                    wg, wb = loaded.pop(u)
                    G, Pj = gb.next(), pb.next()
                    for k in range(16):
                        P.op("pe", lambda e, G=G, wg=wg, k=k, tt=tt: e.matmul(ps[:, G, :], lhsT=wg.t[:, k, :], rhs=hT[:, k, tok(tt)], start=(k == 0), stop=(k == 15)),
                             reads=[wg.r, r_hT[tt]], writes=[r_ps[G]])
                    for kk in range(4):
                        P.op("pe", lambda e, Pj=Pj, wb=wb, kk=kk, b=b, tt=tt: e.matmul(ps[:, Pj, :], lhsT=wb.t[:, kk, :], rhs=oall[:, 4 * b + kk, tok(tt)], start=(kk == 0), stop=(kk == 3)),
                             reads=[wb.r, r_o[4 * b + kk][tt]], writes=[r_ps[Pj]])
                    sg = sgr.next()
                    P.op("act", lambda e, sg=sg, G=G, b=b, d=d: e.activation(out=sg.t[:], in_=ps[:, G, :], func=AF.Sigmoid, bias=cv(base + CV_GB + b * 16 + d), scale=1.0),
                         reads=[r_ps[G], cvec.r], writes=[sg.r])
                    if b == 0:
                        acc = acr.next()
                        P.op("dve", lambda e, acc=acc, Pj=Pj, sg=sg: e.tensor_tensor(out=acc.t[:], in0=ps[:, Pj, :], in1=sg.t[:], op=ALU.mult),
                             reads=[r_ps[Pj], sg.r], writes=[acc.r])
                    else:
                        tm = tmr.next()
                        P.op("dve", lambda e, tm=tm, Pj=Pj, sg=sg: e.tensor_tensor(out=tm.t[:], in0=ps[:, Pj, :], in1=sg.t[:], op=ALU.mult),
                             reads=[r_ps[Pj], sg.r], writes=[tm.r])
                        if b < 3:
                            P.op("dve", lambda e, tm=tm, acc=acc: e.tensor_tensor(out=acc.t[:], in0=acc.t[:], in1=tm.t[:], op=ALU.add),
                                 reads=[tm.r, acc.r], writes=[acc.r])
                        else:
                            P.op("dve", lambda e, tm=tm, acc=acc, d=d: e.tensor_tensor(out=merged.t[:, d, :], in0=acc.t[:], in1=tm.t[:], op=ALU.add),
                                 reads=[tm.r, acc.r], writes=[r_m[d]])
                    if d == 15 and b == 3:
                        ob = Rot([0, 1, 2, 3])
                        for dd in range(16):
                            if dd + 2 < 16:
                                load_wo(dd + 2)
                            wo = wo_loaded.pop(dd)
                            xi, xo = xir.next(), xor_.next()
                            P.op("sp", lambda e, xi=xi, dd=dd, tt=tt: e.dma_start(out=xi.t[:], in_=xsrc[dd * 128:(dd + 1) * 128, tok(tt)]),
                                 reads=[xsrc_res], writes=[xi.r], dma=True)
                            B = ob.next()
                            for k in range(16):
                                P.op("pe", lambda e, B=B, wo=wo, k=k: e.matmul(ps[:, B, :], lhsT=wo.t[:, k, :], rhs=merged.t[:, k, :], start=(k == 0), stop=(k == 15)),
                                     reads=[wo.r, r_m[k]], writes=[r_ps[B]])
                            P.op("dve", lambda e, xo=xo, xi=xi, B=B: e.tensor_tensor(out=xo.t[:], in0=ps[:, B, :], in1=xi.t[:], op=ALU.add),
                                 reads=[r_ps[B], xi.r], writes=[xo.r])
                            P.op("sp", lambda e, xo=xo, dd=dd, tt=tt: e.dma_start(out=xa[dd * 128:(dd + 1) * 128, tok(tt)], in_=xo.t[:]),
                                 reads=[xo.r], writes=[r_xa], dma=True)
                P.barrier()

        def phase_ffn(l):
            with ExitStack() as ph:
                g = Tile(ph, nc, [128, 44, TT], BF16, "fg")
                r_g = [Res(f"g{i}") for i in range(44)]
                war = Ring(ph, nc, 4, [128, 16, 128], BF16, "fwa")
                wbr_ = Ring(ph, nc, 4, [128, 16, 128], BF16, "fwb")
                wdr = Ring(ph, nc, 3, [128, 44, 128], BF16, "fwd")
                car = Ring(ph, nc, 2, [128, TT], F32, "fca")
                cbr = Ring(ph, nc, 2, [128, TT], F32, "fcb")
                sar = Ring(ph, nc, 2, [128, TT], F32, "fsa")
                xir = Ring(ph, nc, 2, [128, TT], F32, "fxi")
                xor_ = Ring(ph, nc, 2, [128, TT], F32, "fxo")
                base = l * CV_L
                FST = int(os.environ.get('FFN_STAGE', 9))
                zb = Rot([(0, 1), (2, 3), (4, 5)])
                hb = Rot([6, 7])
                ob = Rot([0, 1, 2, 3])
                for tt in range(NT):
                    t0 = tt * TT
                    hasL, hasR = tt > 0, tt < NT - 1
                    if hasL and hasR:
                        hcols, hsl = slice(0, 2), slice(t0 - 1, t0 + TT + 1, TT + 1)
                    elif hasL:
                        hcols, hsl = slice(0, 1), slice(t0 - 1, t0)
                    else:
                        hcols, hsl = slice(1, 2), slice(t0 + TT, t0 + TT + 1)
                    htts = sorted({max(tt - 1, 0), min(tt + 1, NT - 1)} - {tt})
                    loaded = {}

                    def load_up(i):
                        wa, wb = war.next(), wbr_.next()
                        for (w_, c) in ((wa, i), (wb, 44 + i)):
                            P.op("sp", lambda e, w_=w_, c=c: e.dma_start(out=w_.t[:], in_=wup[l, c * 128:(c + 1) * 128, :].rearrange("p (k j) -> p k j", k=16)),
                                 reads=[wres[("wup", l, c)]], writes=[w_.r], dma=True)
                        loaded[i] = (wa, wb)

                    PF = 3
                    for i in range(PF):
                        load_up(i)
                    wd_loaded = {}

                    def load_wd(dd):
                        wd = wdr.next()
                        P.op("sp", lambda e, wd=wd, dd=dd: e.dma_start(out=wd.t[:].rearrange("p k j -> p (k j)"), in_=wdn[l, dd * 512:(dd + 1) * 512, :].rearrange("(p f) c -> p (f c)", f=4)),
                             reads=[wres[("wdn", l, dd)]], writes=[wd.r], dma=True)
                        wd_loaded[dd] = wd

                    for i in range(44):
                        if i + PF < 44:
                            load_up(i + PF)
                        if i == 41:
                            for dd in range(2):
                                load_wd(dd)
                        wa, wb = loaded.pop(i)
                        A, B = zb.next()
                        H = hb.next()
                        for (w_, bank) in ((wa, A), (wb, B)):
                            for k in range(16):
                                P.op("pe", lambda e, w_=w_, bank=bank, k=k, tt=tt: e.matmul(ps[:, bank, :], lhsT=w_.t[:, k, :], rhs=hT[:, k, tok(tt)], start=(k == 0), stop=(k == 15)),
                                     reads=[w_.r, r_hT[tt]], writes=[r_ps[bank]])
                        for hi, w_ in enumerate((wa, wb) if FST >= 2 else ()):
                            oc = slice(hi * 2 + hcols.start, hi * 2 + hcols.stop)
                            for k in range(16):
                                P.op("pe", lambda e, w_=w_, H=H, k=k, oc=oc, hsl=hsl: e.matmul(ps[:, H, oc], lhsT=w_.t[:, k, :], rhs=hT[:, k, hsl], start=(k == 0), stop=(k == 15)),
                                     reads=[w_.r] + [r_hT[x] for x in htts], writes=[r_ps[H]])
                        ca, cb, sa = car.next(), cbr.next(), sar.next()
                        if FST < 3:
                            continue
                        for hi, (ct, bank, ch) in enumerate(((ca, A, i), (cb, B, 44 + i))):
                            dw0, dw1, dw2, dwb = (cv(base + CV_FDW + 0 * 88 + ch), cv(base + CV_FDW + 1 * 88 + ch), cv(base + CV_FDW + 2 * 88 + ch), cv(base + CV_FDWB + ch))
                            P.op("act", lambda e, ct=ct, bank=bank, dw1=dw1, dwb=dwb: e.activation(out=ct.t[:], in_=ps[:, bank, :], func=AF.Identity, bias=dwb, scale=dw1),
                                 reads=[r_ps[bank], cvec.r], writes=[ct.r])
                            P.op("dve", lambda e, ct=ct, bank=bank, dw0=dw0: e.scalar_tensor_tensor(out=ct.t[:, 1:TT], in0=ps[:, bank, 0:TT - 1], scalar=dw0, in1=ct.t[:, 1:TT], op0=ALU.mult, op1=ALU.add),
                                 reads=[r_ps[bank], cvec.r, ct.r], writes=[ct.r])
                            P.op("dve", lambda e, ct=ct, bank=bank, dw2=dw2: e.scalar_tensor_tensor(out=ct.t[:, 0:TT - 1], in0=ps[:, bank, 1:TT], scalar=dw2, in1=ct.t[:, 0:TT - 1], op0=ALU.mult, op1=ALU.add),
                                 reads=[r_ps[bank], cvec.r, ct.r], writes=[ct.r])
                            if hasL:
                                P.op("dve", lambda e, ct=ct, H=H, hi=hi, dw0=dw0: e.scalar_tensor_tensor(out=ct.t[:, 0:1], in0=ps[:, H, hi * 2:hi * 2 + 1], scalar=dw0, in1=ct.t[:, 0:1], op0=ALU.mult, op1=ALU.add),
                                     reads=[r_ps[H], cvec.r, ct.r], writes=[ct.r])
                            if hasR:
                                P.op("dve", lambda e, ct=ct, H=H, hi=hi, dw2=dw2: e.scalar_tensor_tensor(out=ct.t[:, TT - 1:TT], in0=ps[:, H, hi * 2 + 1:hi * 2 + 2], scalar=dw2, in1=ct.t[:, TT - 1:TT], op0=ALU.mult, op1=ALU.add),
                                     reads=[r_ps[H], cvec.r, ct.r], writes=[ct.r])
                        P.op("act", lambda e, sa=sa, ca=ca: e.activation(out=sa.t[:], in_=ca.t[:], func=AF.Silu), reads=[ca.r], writes=[sa.r])
                        P.op("dve", lambda e, sa=sa, cb=cb, i=i: e.tensor_tensor(out=g.t[:, i, :], in0=sa.t[:], in1=cb.t[:], op=ALU.mult), reads=[sa.r, cb.r], writes=[r_g[i]])
                    for dd in range(16 if FST >= 4 else 0):
                        if dd + 2 < 16:
                            load_wd(dd + 2)
                        wd = wd_loaded.pop(dd)
                        xi, xo = xir.next(), xor_.next()
                        P.op("sp", lambda e, xi=xi, dd=dd, tt=tt: e.dma_start(out=xi.t[:], in_=xa[dd * 128:(dd + 1) * 128, tok(tt)]),
                             reads=[r_xa], writes=[xi.r], dma=True)
                        Bk = ob.next()
                        for i in range(44):
                            P.op("pe", lambda e, Bk=Bk, wd=wd, i=i: e.matmul(ps[:, Bk, :], lhsT=wd.t[:, i, :], rhs=g.t[:, i, :], start=(i == 0), stop=(i == 43)),
                                 reads=[wd.r, r_g[i]], writes=[r_ps[Bk]])
                        P.op("dve", lambda e, xo=xo, xi=xi, Bk=Bk: e.tensor_tensor(out=xo.t[:], in0=ps[:, Bk, :], in1=xi.t[:], op=ALU.add),
                             reads=[r_ps[Bk], xi.r], writes=[xo.r])
                        P.op("sp", lambda e, xo=xo, dd=dd, tt=tt: e.dma_start(out=xb[dd * 128:(dd + 1) * 128, tok(tt)], in_=xo.t[:]),
                             reads=[xo.r], writes=[r_xb], dma=True)
                P.barrier()

        r_xin = Res("xin")

        def dbg_dump(key, src_ap, reads):
            if dbg:
                P.op("sp", lambda e: e.dma_start(out=dbg_t[key], in_=src_ap), reads=reads, writes=[Res()], dma=True)
                P.barrier()

        order = ["norm", "gm", "sw", "cv", "na", "merge", "ffn"]
        lim = order.index(stop_after) if stop_after is not None else 99

        def schedule():
            if os.environ.get('ONLY_Y'):
                phase_norm(xT[0], r_xin, CV_L * L, 'y', dst=yT[0], dst_res=Res('y'))
                return
            for s in range(NSEQ):
                for l in range(L):
                    first = (s == 0 and l == 0)
                    if l == 0:
                        xsrc, xsrc_res = xT[s], r_xin
                    else:
                        xsrc, xsrc_res = xb, r_xb
                    phase_norm(xsrc, xsrc_res, l * CV_L + CV_N1, "h")
                    if first:
                        issue_casts(0, 1)
                        for ll in range(1, L):
                            issue_casts(ll, 0)
                            issue_casts(ll, 1)
                    if first:
                        dbg_dump("hT", hT[:].rearrange("p k t -> p (k t)"), r_hT)
                    if lim == 0:
                        return
                    with ExitStack() as mx:
                        oall = mx.enter_context(nc.sbuf_tensor(f"oall_{s}_{l}", [128, 16, T], BF16))
                        r_o = [[Res(f"o{c}_{tt}") for tt in range(NT)] for c in range(16)]
                        allo = [r_o[c][tt] for c in range(16) for tt in range(NT)]
                        mixer_gm(l, oall, r_o)
                        if lim >= 2:
                            mixer_sw(l, oall, r_o)
                        if lim >= 3:
                            mixer_cv(l, oall, r_o)
                        if lim >= 4:
                            mixer_na(l, oall, r_o)
                        if first and dbg:
                            nbr = min(max(lim, 1), 4) * 4
                            P.op("sp", lambda e, oall=oall, nbr=nbr: e.dma_start(out=dbg_t["oall"][:, 0:nbr * T], in_=oall[:, 0:nbr, :].rearrange("p k t -> p (k t)")),
                                 reads=allo, writes=[Res()], dma=True)
                            P.barrier()
                        if lim >= 5:
                            phase_merge(l, oall, r_o, xsrc, xsrc_res)
                    if lim < 5:
                        return
                    if first:
                        dbg_dump("xa", xa, [r_xa])
                    if lim == 5:
                        return
                    phase_norm(xa, r_xa, l * CV_L + CV_N2, "h")
                    phase_ffn(l)
                    if first:
                        dbg_dump("xb", xb, [r_xb])
                    if lim == 6:
                        return
                phase_norm(xb, r_xb, CV_L * L, "y", dst=yT[s], dst_res=Res(f"y{s}"))

        schedule()
        stats = P.emit(nc, st)
    return nc, stats


def _tile_major(w, nk):
    K, N = w.shape
    a = w.reshape(nk, 128, N // 128, 128).transpose(2, 1, 0, 3)
    return np.ascontiguousarray(a).reshape((N // 128) * 128, nk * 128)


def prep_shared(inp, L):
    f = lambda k: np.asarray(inp[k], dtype=np.float32)
    w_in = f("w_in")
    cols = []
    cols += list(range(0, 512))
    cols += list(range(512, 1024))
    qbase = 1024
    q_cols, qp_cols = [], []
    for h in PERM_HEADS:
        for d in range(64):
            q_cols.append(qbase + h * 64 + d)
            qp_cols.append(qbase + h * 64 + (d + 32) % 64)
    cols += q_cols + qp_cols
    kbase = 1536
    cols += [kbase + i for i in range(128)]
    cols += [kbase + (i // 64) * 64 + ((i % 64) + 32) % 64 for i in range(128)]
    cols += list(range(1664, 1792))
    cols += list(range(1792, 2304))
    cols += list(range(2304, 2816))
    cols += list(range(2816, 3328))
    cols += list(range(3328, 3840))
    cols += list(range(3840, 4352))
    cols += list(range(4352, 4352 + 8192))
    cols = np.asarray(cols)
    assert cols.size == NCW * 128
    out = {}
    out["win"] = np.stack([_tile_major(w_in[l][:, cols], 16) for l in range(L)])
    wb = f("w_branch")
    rowperm = np.asarray([PERM_HEADS[(p // 64)] * 64 + p % 64 for p in range(512)])
    wbr_l = []
    for l in range(L):
        parts = []
        for b in range(4):
            w = wb[l, b]
            if b == 1:
                w = w[rowperm]
            parts.append(_tile_major(w, 4))
        wbr_l.append(np.concatenate(parts, axis=0))
    out["wbr"] = np.stack(wbr_l)
    out["wout"] = np.stack([_tile_major(f("w_out")[l], 16) for l in range(L)])
    out["wup"] = np.stack([_tile_major(f("w_up")[l], 16) for l in range(L)])
    out["wdn"] = np.stack([_tile_major(f("w_down")[l], 44).reshape(16 * 128 * 4, 1408) for l in range(L)])
    cvec = np.zeros((128, CV_L * L + 16), np.float32)
    pm = lambda v, n: np.asarray(v, np.float32).reshape(n, 128).T
    for l in range(L):
        b0 = l * CV_L
        cvec[:, b0 + CV_N1:b0 + CV_N1 + 16] = pm(f("norm1_g")[l], 16)
        cvec[:, b0 + CV_N2:b0 + CV_N2 + 16] = pm(f("norm2_g")[l], 16)
        cvec[:, b0 + CV_GB:b0 + CV_GB + 64] = pm(f("gate_b")[l].reshape(-1), 64)
        cvec[:, b0 + CV_CDWB:b0 + CV_CDWB + 4] = pm(f("cv_dwb")[l], 4)
        cvec[:, b0 + CV_CLG:b0 + CV_CLG + 4] = pm(f("cv_ln_g")[l], 4)
        cvec[:, b0 + CV_CLB:b0 + CV_CLB + 4] = pm(f("cv_ln_b")[l], 4)
        cvec[:, b0 + CV_CDW:b0 + CV_CDW + 124] = pm(f("cv_dw")[l].reshape(-1), 124)
        cvec[:, b0 + CV_FDW:b0 + CV_FDW + 264] = pm(f("ffn_dw")[l].reshape(-1), 264)
        cvec[:, b0 + CV_FDWB:b0 + CV_FDWB + 88] = pm(f("ffn_dwb")[l], 88)
        sk = f("sw_sink")[l]
        for j in range(4):
            cvec[0:64, b0 + CV_SINK + j] = sk[j]
            cvec[64:128, b0 + CV_SINK + j] = sk[4 + j]
    cvec[:, CV_L * L:CV_L * L + 16] = pm(f("final_g"), 16)
    out["cvec"] = cvec
    gm = np.zeros((L, 128, 6, 512), np.float32)
    for l in range(L):
        gm[l, :, 0, :] = f("gm_ln_g")[l][None, :]
        gm[l, :, 1, :] = f("gm_ln_b")[l][None, :]
        for g in range(4):
            gm[l, :, 2 + g, :] = np.tile(f("gm_bs")[l, g], 4)[None, :]
    out["gmrep"] = gm
    out["wsT"] = np.ascontiguousarray(f("gm_ws")[:L].transpose(0, 3, 1, 2)).reshape(L, 128, 512)
    p = np.arange(128)
    inv = np.power(np.float32(10000.0), -(np.arange(32, dtype=np.float32)) / np.float32(32)).astype(np.float32)
    ang = np.arange(T, dtype=np.float32)[None, :] * inv[p % 32][:, None]
    sign = np.where((p % 64) < 32, -1.0, 1.0).astype(np.float32)[:, None]
    out["rope"] = np.stack([np.cos(ang), sign * np.sin(ang)], axis=1).astype(np.float32)
    kk = np.arange(128)[:, None]
    qq = np.arange(128)[None, :]
    m = np.zeros((128, 3, 128), np.float32)
    m[:, 0, :] = np.where(kk >= qq, 0.0, NEG)
    m[:, 2, :] = np.where(kk <= qq, 0.0, NEG)
    out["swmask"] = m.reshape(128, 384)
    rpb = f("na_rpb")
    kc = (np.arange(128) % 64)[:, None]
    qc = np.arange(64)[None, :]
    up = (np.arange(128) >= 64).astype(np.int64)
    dc = np.clip(kc - qc, -15, 15) + 15
    ii = np.clip(np.arange(15)[None, :] + up[:, None], 0, 14)
    nag = np.zeros((L, 4, 128, 2, 15, 64), np.float32)
    for l in range(L):
        for j in range(4):
            for hh in range(2):
                nag[l, j, :, hh] = rpb[l, 2 * j + hh][ii[:, :, None], dc[:, None, :]]
    out["nag"] = nag.reshape(L, 4, 128, 1920)
    cs = np.clip(qc - 8, 0, 48)
    ok = (kc >= cs) & (kc < cs + 16)
    mk = np.where(ok, 0.0, NEG).astype(np.float32)
    out["ident"] = np.eye(128, dtype=np.float32)
    out["namask"] = np.ascontiguousarray(np.broadcast_to(mk[:, None, None, :], (128, 2, 15, 64))).reshape(128, 1920)
    return out


SAMPLE_WINDOWS = [(0, 0, 1599), (896, 1599, 2495), (1792, 2495, 3391), (2048, 3391, 4096)]

_CACHE = {}


def kernel(**inputs):
    L = 2
    n_cores = 8
    xp = np.asarray(inputs["x_prompt"], np.float32)
    xs = np.asarray(inputs["x_sample"], np.float32)
    shared = prep_shared(inputs, L)
    if "nc" not in _CACHE:
        _CACHE["nc"] = build_nc(3, L)[0]
    nc = _CACHE["nc"]
    in_maps = []
    for c in range(n_cores):
        sb, w = c // 4, c % 4
        st = SAMPLE_WINDOWS[w][0]
        seqs = [xp[2 * c], xp[2 * c + 1], xs[sb, st:st + T]]
        xT = np.ascontiguousarray(np.stack([s.T for s in seqs]))
        m = dict(shared)
        m["xT"] = xT
        in_maps.append(m)
    res = run_bass_kernel_spmd(nc, in_maps, core_ids=list(range(n_cores)))
    y_prompt = np.empty_like(xp)
    y_sample = np.empty_like(xs)
    for c in range(n_cores):
        yT = res.results[c]["yT"]
        y_prompt[2 * c] = yT[0].T
        y_prompt[2 * c + 1] = yT[1].T
        sb, w = c // 4, c % 4
        st, lo, hi = SAMPLE_WINDOWS[w]
        y_sample[sb, lo:hi] = yT[2].T[lo - st:hi - st]
    return (y_prompt, y_sample)
```
